# Optimizing a Trainium2 kernel written in Bass

```python
import math
import jax, jax.numpy as jnp
from jax import lax
import numpy as np

D_MODEL = 2048
BATCH = 2
SEQ = 8192
DEPTH = 4
DEC_BATCH = 1
DEC_SEQ = 16384
PAST_LEN = 128

N_MIXERS = 2
N_CONV_LAYERS = (DEPTH + 1) // 2
N_MLA_LAYERS = DEPTH // 2
CONV_EXPAND = 2
CONV_WIDTH = CONV_EXPAND * D_MODEL
CONV_KERNEL = 31
CONV_IN = 3 * CONV_WIDTH
N_HEADS = 16
QK_NOPE_DIM = 128
QK_ROPE_DIM = 64
V_HEAD_DIM = 128
Q_LORA_RANK = 512
KV_LORA_RANK = 512
MLA_WIDTH = N_HEADS * V_HEAD_DIM
MLA_IN = Q_LORA_RANK + KV_LORA_RANK + QK_ROPE_DIM + MLA_WIDTH
Q_BLOCK = 128
ROPE_THETA = 10000.0
EPS = 1e-6

kernel_name = "hybrid_conv_mla_adaln_encoder"


def rmsnorm(x, g):
    xf = x.astype(jnp.float32)
    y = xf * lax.rsqrt(jnp.mean(xf * xf, axis=-1, keepdims=True) + EPS)
    return (y * g.astype(jnp.float32)).astype(x.dtype)


def layernorm(x, g, b):
    xf = x.astype(jnp.float32)
    mu = jnp.mean(xf, axis=-1, keepdims=True)
    xc = xf - mu
    var = jnp.mean(xc * xc, axis=-1, keepdims=True)
    y = xc * lax.rsqrt(var + EPS) * g.astype(jnp.float32) + b.astype(jnp.float32)
    return y.astype(x.dtype)


def rope_tables(length):
    inv = 1.0 / (ROPE_THETA ** (jnp.arange(0, QK_ROPE_DIM, 2, dtype=jnp.float32) / QK_ROPE_DIM))
    ang = jnp.arange(length, dtype=jnp.float32)[:, None] * inv[None, :]
    return jnp.cos(ang), jnp.sin(ang)


def apply_rope(x, cos, sin):
    xf = x.astype(jnp.float32)
    x1, x2 = jnp.split(xf, 2, axis=-1)
    return jnp.concatenate([x1 * cos - x2 * sin, x2 * cos + x1 * sin], axis=-1).astype(x.dtype)


def depthwise_conv(x, w, b):
    pad = CONV_KERNEL // 2
    y = lax.conv_general_dilated(x, w[:, None, :].astype(x.dtype), window_strides=(1,),
                                 padding=[(pad, pad)], dimension_numbers=('NWC', 'WIO', 'NWC'),
                                 feature_group_count=x.shape[-1])
    return y + b.astype(x.dtype)


def conv_mixer(h, w_in, conv_w, conv_b, ln_g, ln_b, w_out):
    u = jnp.einsum('bld,de->ble', h, w_in)
    a, gl, z = jnp.split(u, 3, axis=-1)
    y = a * jax.nn.sigmoid(gl)
    y = depthwise_conv(y, conv_w, conv_b)
    y = jax.nn.silu(layernorm(y, ln_g, ln_b))
    y = y * jax.nn.silu(z)
    return jnp.einsum('blc,cd->bld', y, w_out)


def mla_attention(q_nope, q_rope, k_nope, k_rope, v):
    bsz, length = q_nope.shape[0], q_nope.shape[1]
    nb = length // Q_BLOCK
    scale = 1.0 / math.sqrt(QK_NOPE_DIM + QK_ROPE_DIM)

    def blocks(t):
        return t.reshape(bsz, nb, Q_BLOCK, *t.shape[2:]).swapaxes(0, 1)

    def one_block(args):
        qn, qr = args
        s = (jnp.einsum('bqhd,bkhd->bhqk', qn, k_nope)
             + jnp.einsum('bqhr,bkr->bhqk', qr, k_rope))
        p = jax.nn.softmax(s.astype(jnp.float32) * scale, axis=-1).astype(v.dtype)
        return jnp.einsum('bhqk,bkhd->bqhd', p, v)

    o = lax.map(one_block, (blocks(q_nope), blocks(q_rope)))
    return o.swapaxes(0, 1).reshape(bsz, length, MLA_WIDTH)


def mla_mixer(h, w_in, q_norm, kv_norm, w_uq, w_ukv, w_out, cos, sin):
    bsz, length = h.shape[0], h.shape[1]
    u = jnp.einsum('bld,de->ble', h, w_in)
    c_q, c_kv, k_r, z = jnp.split(
        u, [Q_LORA_RANK, Q_LORA_RANK + KV_LORA_RANK, Q_LORA_RANK + KV_LORA_RANK + QK_ROPE_DIM], axis=-1)
    c_q = rmsnorm(c_q, q_norm)
    c_kv = rmsnorm(c_kv, kv_norm)
    q = jnp.einsum('blr,re->ble', c_q, w_uq).reshape(bsz, length, N_HEADS, QK_NOPE_DIM + QK_ROPE_DIM)
    kv = jnp.einsum('blr,re->ble', c_kv, w_ukv).reshape(bsz, length, N_HEADS, QK_NOPE_DIM + V_HEAD_DIM)
    q_nope, q_rope = q[..., :QK_NOPE_DIM], q[..., QK_NOPE_DIM:]
    k_nope, v = kv[..., :QK_NOPE_DIM], kv[..., QK_NOPE_DIM:]
    q_rope = apply_rope(q_rope, cos[:, None, :], sin[:, None, :])
    k_rope = apply_rope(k_r, cos, sin)
    o = mla_attention(q_nope, q_rope, k_nope, k_rope, v)
    o = o * jax.nn.silu(z)
    return jnp.einsum('ble,ed->bld', o, w_out)


def trunk(x, c, norm_g, ada_w, ada_b,
          conv_w_in, conv_dw_w, conv_dw_b, conv_ln_g, conv_ln_b, conv_w_out,
          mla_w_in, mla_q_norm, mla_kv_norm, mla_w_uq, mla_w_ukv, mla_w_out, final_g):
    cos, sin = rope_tables(x.shape[1])
    c_act = jax.nn.silu(c)
    for i in range(DEPTH):
        mod = jnp.einsum('bd,de->be', c_act, ada_w[i]) + ada_b[i]
        shift, scale, gate = jnp.split(mod, 3, axis=-1)
        h = rmsnorm(x, norm_g[i]) * (1.0 + scale[:, None, :]) + shift[:, None, :]
        j = i // N_MIXERS
        if i % N_MIXERS == 0:
            y = conv_mixer(h, conv_w_in[j], conv_dw_w[j], conv_dw_b[j], conv_ln_g[j], conv_ln_b[j], conv_w_out[j])
        else:
            y = mla_mixer(h, mla_w_in[j], mla_q_norm[j], mla_kv_norm[j], mla_w_uq[j], mla_w_ukv[j], mla_w_out[j], cos, sin)
        x = x + gate[:, None, :] * y
    return rmsnorm(x, final_g)


def setup_inputs(seed: int = 0) -> dict:
    key = jax.random.key(seed)
    ks = jax.random.split(key, 24)
    f32 = jnp.float32

    def nrm(k, shape, std):
        return jax.random.normal(k, shape, f32) * std

    return {
        "x_prompt": nrm(ks[0], (BATCH, SEQ, D_MODEL), 1.0),
        "x_sample": nrm(ks[1], (DEC_BATCH, DEC_SEQ, D_MODEL), 1.0),
        "c_prompt": nrm(ks[2], (BATCH, D_MODEL), 1.0),
        "c_sample": nrm(ks[3], (DEC_BATCH, D_MODEL), 1.0),
        "norm_g": 1.0 + nrm(ks[4], (DEPTH, D_MODEL), 0.02),
        "ada_w": nrm(ks[5], (DEPTH, D_MODEL, 3 * D_MODEL), 0.5 * D_MODEL ** -0.5),
        "ada_b": nrm(ks[6], (DEPTH, 3 * D_MODEL), 0.01),
        "conv_w_in": nrm(ks[7], (N_CONV_LAYERS, D_MODEL, CONV_IN), D_MODEL ** -0.5),
        "conv_dw_w": nrm(ks[8], (N_CONV_LAYERS, CONV_KERNEL, CONV_WIDTH), CONV_KERNEL ** -0.5),
        "conv_dw_b": nrm(ks[9], (N_CONV_LAYERS, CONV_WIDTH), 0.01),
        "conv_ln_g": 1.0 + nrm(ks[10], (N_CONV_LAYERS, CONV_WIDTH), 0.02),
        "conv_ln_b": nrm(ks[11], (N_CONV_LAYERS, CONV_WIDTH), 0.01),
        "conv_w_out": nrm(ks[12], (N_CONV_LAYERS, CONV_WIDTH, D_MODEL), CONV_WIDTH ** -0.5),
        "mla_w_in": nrm(ks[13], (N_MLA_LAYERS, D_MODEL, MLA_IN), D_MODEL ** -0.5),
        "mla_q_norm": 1.0 + nrm(ks[14], (N_MLA_LAYERS, Q_LORA_RANK), 0.02),
        "mla_kv_norm": 1.0 + nrm(ks[15], (N_MLA_LAYERS, KV_LORA_RANK), 0.02),
        "mla_w_uq": nrm(ks[16], (N_MLA_LAYERS, Q_LORA_RANK, N_HEADS * (QK_NOPE_DIM + QK_ROPE_DIM)), Q_LORA_RANK ** -0.5),
        "mla_w_ukv": nrm(ks[17], (N_MLA_LAYERS, KV_LORA_RANK, N_HEADS * (QK_NOPE_DIM + V_HEAD_DIM)), KV_LORA_RANK ** -0.5),
        "mla_w_out": nrm(ks[18], (N_MLA_LAYERS, MLA_WIDTH, D_MODEL), MLA_WIDTH ** -0.5),
        "final_g": 1.0 + nrm(ks[19], (D_MODEL,), 0.02),
    }


def reference(x_prompt, x_sample, c_prompt, c_sample, norm_g, ada_w, ada_b,
              conv_w_in, conv_dw_w, conv_dw_b, conv_ln_g, conv_ln_b, conv_w_out,
              mla_w_in, mla_q_norm, mla_kv_norm, mla_w_uq, mla_w_ukv, mla_w_out, final_g):
    y_prompt = trunk(x_prompt, c_prompt, norm_g, ada_w, ada_b,
                     conv_w_in, conv_dw_w, conv_dw_b, conv_ln_g, conv_ln_b, conv_w_out,
                     mla_w_in, mla_q_norm, mla_kv_norm, mla_w_uq, mla_w_ukv, mla_w_out, final_g)
    y_sample = trunk(x_sample, c_sample, norm_g, ada_w, ada_b,
                     conv_w_in, conv_dw_w, conv_dw_b, conv_ln_g, conv_ln_b, conv_w_out,
                     mla_w_in, mla_q_norm, mla_kv_norm, mla_w_uq, mla_w_ukv, mla_w_out, final_g)
    return (y_prompt, y_sample)
```

```python
import math
import numpy as np
import ml_dtypes
import concourse.bass as bass
import concourse.mybir as mybir
from concourse.bass_utils import run_bass_kernel_spmd

F32 = mybir.dt.float32
BF16 = mybir.dt.bfloat16
AF = mybir.ActivationFunctionType
ALU = mybir.AluOpType

NCORES = 8
D = 2048
KD = 16
CW = 4096
CC = 32
NH = 16
SEGT = 2048
HALO = 15
PADT = SEGT + 2 * HALO
NB = 256
QB = 512
EPS = 1e-6
SCALE = 1.0 / math.sqrt(192.0)
LAT = 576


class Slot:
    __slots__ = ("name", "lw", "lws", "rd")

    def __init__(self, name=""):
        self.name = name
        self.lw = None
        self.lws = []
        self.rd = []


class Op:
    __slots__ = ("eng", "fn", "deps", "dma", "lane", "val", "idx", "cc")


N_LANES = {"sp": 16, "act": 2, "pool": 8}
ENGS = ("pe", "act", "dve", "pool", "sp")


class Prog:
    def __init__(self):
        self.ops = []
        self.eng_ops = {e: [] for e in ENGS}
        self.eng_cnt = {e: 0 for e in ENGS}
        self.lane_cnt = {e: [0] * n for e, n in N_LANES.items()}
        self.lane_rr = {e: 0 for e in N_LANES}
        self.floor = None
        self.n_cc = 0

    def op(self, eng, fn, reads=(), writes=(), dma=False, extra=(), awrites=(), cc=False):
        o = Op()
        o.eng = eng
        o.fn = fn
        o.dma = dma
        o.idx = len(self.ops)
        deps = set(extra)
        if self.floor is not None:
            deps.add(self.floor)
        for s in reads:
            if s.lw is not None:
                deps.add(s.lw)
            deps.update(s.lws)
        for s in writes:
            if s.lw is not None:
                deps.add(s.lw)
            deps.update(s.lws)
            deps.update(s.rd)
        for s in awrites:
            if s.lw is not None:
                deps.add(s.lw)
            deps.update(s.rd)
        o.deps = deps
        for s in reads:
            s.rd.append(o.idx)
        for s in writes:
            s.lw = o.idx
            s.lws = []
            s.rd = []
        for s in awrites:
            s.lws.append(o.idx)
        o.cc = None
        if cc:
            o.cc = self.n_cc
            self.n_cc += 1
            o.lane = None
            o.val = 1
        elif dma:
            l = self.lane_rr[eng]
            self.lane_rr[eng] = (l + 1) % N_LANES[eng]
            self.lane_cnt[eng][l] += 1
            o.lane = l
            o.val = 16 * self.lane_cnt[eng][l]
        else:
            self.eng_cnt[eng] += 1
            o.lane = None
            o.val = self.eng_cnt[eng]
        self.ops.append(o)
        self.eng_ops[eng].append(o)
        return o

    def barrier(self, fn):
        extra = []
        for e in ENGS:
            need_c = e != "sp"
            nl = N_LANES.get(e, 0) if e != "pool" else 0
            lanes = set()
            for o in reversed(self.eng_ops[e]):
                if o.dma:
                    if e != "pool" and o.lane not in lanes:
                        lanes.add(o.lane)
                        extra.append(o.idx)
                elif o.cc is not None:
                    continue
                elif need_c:
                    need_c = False
                    extra.append(o.idx)
                if not need_c and len(lanes) == nl:
                    break
        b = self.op("dve", fn, extra=extra)
        self.floor = b.idx
        return b

    def emit(self, block, sems, lane_sems, cc_sems):
        ops = self.ops

        def run(engname, eng):
            waited = {}

            def wait(key, sem, val):
                if waited.get(key, 0) >= val:
                    return
                waited[key] = val
                eng.wait_ge(sem, val)

            for o in self.eng_ops[engname]:
                for d in sorted(o.deps):
                    p = ops[d]
                    if p.cc is not None:
                        wait(("cc", p.cc), cc_sems[p.cc], 1)
                    elif p.dma:
                        wait((p.eng, p.lane), lane_sems[p.eng][p.lane], p.val)
                    else:
                        if p.eng == "pe" and engname == "pe" and not o.dma:
                            continue
                        wait(p.eng, sems[p.eng], p.val)
                if o.cc is not None:
                    o.fn(eng).then_inc(cc_sems[o.cc], 1)
                elif o.dma:
                    if o.val > 16:
                        wait((o.eng, o.lane), lane_sems[o.eng][o.lane], o.val - 16)
                    o.fn(eng).then_inc(lane_sems[o.eng][o.lane], 16)
                else:
                    o.fn(eng).then_inc(sems[o.eng], 1)
            if engname in N_LANES:
                for l in range(N_LANES[engname]):
                    c = self.lane_cnt[engname][l]
                    if c:
                        wait((engname, l), lane_sems[engname][l], 16 * c)
            if engname == "pool":
                for i in range(self.n_cc):
                    wait(("cc", i), cc_sems[i], 1)

        block.sync(lambda e: run("sp", e))
        block.tensor(lambda e: run("pe", e))
        block.scalar(lambda e: run("act", e))
        block.vector(lambda e: run("dve", e))
        block.gpsimd(lambda e: run("pool", e))


def mm_group(e, out, pairs):
    n = len(pairs)
    ins = None
    for i, (l, r) in enumerate(pairs):
        ins = e.matmul(out, l, r, start=(i == 0), stop=(i == n - 1))
    return ins


def build(stop=None, debug=()):
    nc = bass.Bass("TRN2", target_bir_lowering=False)
    P = Prog()

    def din(name, shape, dt=F32):
        return nc.dram_tensor(name, list(shape), dt, kind="ExternalInput").ap()

    def dscr(name, shape, dt):
        if name in debug:
            return nc.dram_tensor(name, list(shape), dt, kind="ExternalOutput").ap()
        return nc.dram_tensor(name, list(shape), dt).ap()

    xin = din("xin", [2, PADT, D])
    cT3_d = din("cT3", [128, KD, 3])
    adaw_d = din("adaw", [4, D, 768])
    adab_d = din("adab", [128, 4, 6])
    ng_d = din("ng", [128, 4, KD])
    fg_d = din("fg", [128, KD])
    dw_d = din("dw", [128, 2, CC, 31])
    dwb_d = din("dwb", [128, 2, CC])
    lng_d = din("lng", [128, 2, CC])
    lnb_d = din("lnb", [128, 2, CC])
    qn_d = din("qn", [128, 2, 4])
    kvn_d = din("kvn", [128, 2, 4])
    hmask_d = din("hmask", [128, 4])
    selb_d = din("selb", [128, 2, 2, 8])
    selp_d = din("selp", [128, 2])
    cos_d = din("cos2", [64, 2 * SEGT])
    sin_d = din("sin2s", [64, 2 * SEGT])
    cwin_d = din("conv_w_in", [2, D // 8, 3 * CW])
    cwout_d = din("conv_w_out", [2, CW // 8, D])
    mwin_d = din("mla_w_in", [2, D // 8, 3136])
    wuq_d = din("mla_w_uq", [2, 64, 3072])
    wukv_d = din("mla_w_ukv", [2, 64, 4096])
    mwout_d = din("mla_w_out", [2, D // 8, D])
    yout = nc.dram_tensor("yout", [2, SEGT, D], F32, kind="ExternalOutput").ap()

    cwin_b = [dscr(f"cwin_b{l}", [D, 3 * CW], BF16) for l in range(2)]
    cwout_b = [dscr(f"cwout_b{l}", [CW, D], BF16) for l in range(2)]
    mwin_b = [dscr(f"mwin_b{l}", [D, 3136], BF16) for l in range(2)]
    wuq_b = [dscr(f"wuq_b{l}", [512, 3072], BF16) for l in range(2)]
    wukv_b = [dscr(f"wukv_b{l}", [512, 4096], BF16) for l in range(2)]
    mwout_b = [dscr(f"mwout_b{l}", [D, D], BF16) for l in range(2)]
    xT_a = [dscr(f"xT_a{s}", [D, PADT], F32) for s in range(2)]
    xT_b = [dscr(f"xT_b{s}", [D, PADT], F32) for s in range(2)]
    x1T = dscr("x1T", [D, 2 * SEGT], F32)
    x3T = dscr("x3T", [D, 2 * SEGT], F32)
    cqT = dscr("cqT", [512, 2 * SEGT], BF16)
    szT = dscr("szT", [D, 2 * SEGT], BF16)
    oT = dscr("oT", [D, 2 * SEGT], BF16)
    lat_in = [[dscr(f"lat_in{m}_{s}", [LAT, SEGT], BF16) for s in range(2)] for m in range(2)]
    lat_all = [[dscr(f"lat_all{m}_0", [8 * LAT, SEGT], BF16), dscr(f"lat_all{m}_1", [8 * LAT, SEGT], BF16)]
               for m in range(2)]
    lat_sel = [dscr(f"lat_sel{m}", [4 * LAT, SEGT], BF16) for m in range(2)]
    edge_in = [dscr(f"edge_in{s}", [D, 32], F32) for s in range(2)]
    edge_all = [dscr("edge_all0", [8 * D, 32], F32), dscr("edge_all1", [8 * D, 32], F32)]
    mod_in = dscr("mod_in", [128, 72], F32)
    mod_all = dscr("mod_all", [8 * 128, 72], F32)
    GROUPS = [[list(range(8))], [list(range(8))]]
    NR = [8, 4]

    def fm(ap, c0, c1):
        return ap.rearrange("(k p) t -> p k t", p=128)[:, :, c0:c1]

    uid = [0]

    class Arena:
        def __init__(self, base):
            self.off = base

        def t(self, shape, dt):
            nbytes = int(np.prod(shape[1:])) * (4 if dt == F32 else 2)
            self.off = (self.off + 31) // 32 * 32
            uid[0] += 1
            h = nc.alloc_sbuf_tensor_at(f"t{uid[0]}", list(shape), dt, offset=self.off)
            self.off += nbytes
            assert self.off <= 224 * 1024 - 256, (self.off, shape)
            return h

    CA = Arena(16384 + 256)
    ident = CA.t([128, 128], F32)
    ones = CA.t([128, 128], BF16)
    ones_f = CA.t([128, 128], F32)
    ident_b = CA.t([128, 128], BF16)
    eps_t = CA.t([128, 1], F32)
    gm = CA.t([128, 4, KD, 2], F32)
    sh = CA.t([128, 4, KD, 2], F32)
    gt = CA.t([128, 4, KD, 2], F32)
    ng = CA.t([128, 4, KD], F32)
    fg = CA.t([128, KD], F32)
    dw = CA.t([128, 2, CC, 31], F32)
    dwb = CA.t([128, 2, CC], F32)
    lng = CA.t([128, 2, CC], F32)
    lnb = CA.t([128, 2, CC], F32)
    qn_t = CA.t([128, 2, 4], F32)
    kvn_t = CA.t([128, 2, 4], F32)
    hmask = CA.t([128, 4], F32)
    selb = CA.t([128, 2, 2, 8], F32)
    selp = CA.t([128, 2], F32)
    CBASE = CA.off
    S_const = Slot("const")
    S_mod = Slot("mod")

    psb = [nc.alloc_psum_tensor(f"psb{i}", [128, 512], F32) for i in range(8)]
    S_ps = [Slot(f"ps{i}") for i in range(8)]

    def ld(dst, src, reads=(), writes=(), eng="sp", aw=()):
        return P.op(eng, lambda e: e.dma_start(out=dst, in_=src), reads=reads, writes=writes, dma=True, awrites=aw)

    for (t_, d_) in ((ng, ng_d), (fg, fg_d), (dw, dw_d), (dwb, dwb_d), (lng, lng_d), (lnb, lnb_d),
                     (qn_t, qn_d), (kvn_t, kvn_d), (hmask, hmask_d), (selb, selb_d), (selp, selp_d)):
        ld(t_[:], d_, writes=[S_const])

    def mk_const(e):
        e.memset(ident[:], 0.0)
        e.memset(ones[:], 1.0)
        e.memset(ones_f[:], 1.0)
        e.memset(eps_t[:], EPS)
        e.affine_select(out=ident[:], in_=ident[:], pattern=[[-1, 128]], compare_op=ALU.not_equal,
                        fill=1.0, base=0, channel_multiplier=1)
        return e.tensor_copy(out=ident_b[:], in_=ident[:])
    P.op("pool", mk_const, writes=[S_const])

    W_slots = {}

    def cast_w(key, dst, src, rows, step):
        sh = dscr(f"sh_{key[0]}{key[1]}", [rows, src.shape[1]], BF16)
        sl = []
        for r0 in range(0, rows, step):
            ssl_ = Slot()
            ld(sh[r0:r0 + step, :], src[r0:r0 + step, :], writes=[ssl_], eng="pool")
            sl.append(ssl_)
        full = Slot()
        P.op("pool", lambda e: e.collective_compute("AllGather", ALU.bypass, replica_groups=GROUPS[0],
                                                    ins=[sh.opt()], outs=[dst.opt()]),
             reads=sl, writes=[full], cc=True)
        W_slots[key] = [full]

    def cast_rest():
        for l in range(2):
            if l == 1:
                cast_w(("cwin", l), cwin_b[l], cwin_d[l], D // 8, 128)
            cast_w(("cwout", l), cwout_b[l], cwout_d[l], CW // 8, 512)
            cast_w(("mwin", l), mwin_b[l], mwin_d[l], D // 8, 256)
            cast_w(("wuq", l), wuq_b[l], wuq_d[l], 64, 64)
            cast_w(("wukv", l), wukv_b[l], wukv_d[l], 64, 64)
            cast_w(("mwout", l), mwout_b[l], mwout_d[l], D // 8, 256)
    cast_w(("cwin", 0), cwin_b[0], cwin_d[0], D // 8, 128)

    A0 = Arena(CBASE)
    cT3 = A0.t([128, KD, 3], F32)
    cact = A0.t([128, KD, 3], F32)
    adat = [A0.t([128, KD, 768], F32) for _ in range(2)]
    modp = A0.t([128, 4, 6, 3], F32)
    adab = A0.t([128, 4, 6], F32)
    modall = A0.t([128, 8, 72], F32)
    modsel = A0.t([128, 4, 48, 2], F32)
    xrow = [A0.t([128, D], F32) for _ in range(2)]
    xst = [A0.t([128, KD, 128], F32) for _ in range(2)]
    S_cT3, S_cact, S_modp, S_adab, S_modall, S_modsel = (Slot() for _ in range(6))
    S_adat = [Slot(), Slot()]
    S_modin, S_modalld = Slot(), Slot()
    ld(cT3[:], cT3_d, writes=[S_cT3])
    ld(adab[:], adab_d, writes=[S_adab])
    P.op("act", lambda e: e.activation(out=cact[:], in_=cT3[:], func=AF.Silu), reads=[S_cT3], writes=[S_cact])
    for l in range(4):
        b = l % 2
        ld(adat[b][:], adaw_d[l].rearrange("(k p) e -> p k e", p=128), writes=[S_adat[b]])

        def f(e, l=l, b=b):
            ins = None
            for e6 in range(6):
                ins = mm_group(e, psb[4][:, (l * 6 + e6) * 3:(l * 6 + e6) * 3 + 3],
                               [(adat[b][:, k, e6 * 128:(e6 + 1) * 128], cact[:, k, :]) for k in range(KD)])
            return ins
        P.op("pe", f, reads=[S_adat[b], S_cact], writes=[S_ps[4]])
    P.op("dve", lambda e: e.tensor_tensor(
        out=modp[:], in0=psb[4][:, 0:72].rearrange("p (l e s) -> p l e s", l=4, e=6),
        in1=adab[:].unsqueeze(3).to_broadcast([128, 4, 6, 3]), op=ALU.add),
        reads=[S_ps[4], S_adab], writes=[S_modp])
    ld(mod_in, modp[:].rearrange("p l e s -> p (l e s)"), reads=[S_modp], writes=[S_modin])
    P.op("pool", lambda e: e.collective_compute("AllGather", ALU.bypass, replica_groups=GROUPS[0],
                                                ins=[mod_in.opt()], outs=[mod_all.opt()]),
         reads=[S_modin], writes=[S_modalld])
    ld(modall[:], mod_all.rearrange("(r p) f -> p r f", p=128), reads=[S_modalld], writes=[S_modall])
    cast_rest()

    def f_modsel(e):
        mv = modall[:].rearrange("p r (l e s) -> p r l e s", l=4, e=6)
        ins = None
        for l in range(4):
            dst = modsel[:, l, :, :].rearrange("p (r e) s -> p r e s", r=8)
            e.tensor_copy(out=dst[:, :, :, 0], in_=mv[:, :, l, :, 0])
            e.tensor_scalar(out=dst[:, :, :, 1], in0=mv[:, :, l, :, 1], scalar1=selp[:, 0:1], scalar2=None,
                            op0=ALU.mult)
            ins = e.scalar_tensor_tensor(out=dst[:, :, :, 1], in0=mv[:, :, l, :, 2], scalar=selp[:, 1:2],
                                         in1=dst[:, :, :, 1], op0=ALU.mult, op1=ALU.add)
        return ins
    P.op("dve", f_modsel, reads=[S_modall, S_const], writes=[S_modsel])

    def f_mod2(e):
        e.tensor_copy(out=sh[:], in_=modsel[:, :, 0:16, :])
        e.tensor_copy(out=gt[:], in_=modsel[:, :, 32:48, :])
        e.tensor_scalar(out=gm[:], in0=modsel[:, :, 16:32, :], scalar1=1.0, scalar2=None, op0=ALU.add)
        return e.tensor_tensor(out=gm[:], in0=gm[:], in1=ng[:].unsqueeze(3).to_broadcast([128, 4, KD, 2]),
                               op=ALU.mult)
    P.op("dve", f_mod2, reads=[S_modsel, S_const], writes=[S_mod])

    S_xrow = [Slot(), Slot()]
    S_xst = [[Slot() for _ in range(4)] for _ in range(2)]
    S_xTa = [Slot(), Slot()]
    cnt = 0
    for s in range(2):
        for r0 in range(0, PADT, 128):
            nr = min(128, PADT - r0)
            b = cnt % 2
            cnt += 1
            ld(xrow[b][:nr, :], xin[s, r0:r0 + nr, :], writes=[S_xrow[b]])
            for g in range(4):
                pb = g % 4

                def ft(e, b=b, g=g, nr=nr, pb=pb):
                    ins = None
                    for i in range(4):
                        k = g * 4 + i
                        ins = e.transpose(out=psb[pb][:, i * 128:i * 128 + nr],
                                          in_=xrow[b][:nr, k * 128:(k + 1) * 128], identity=ident[:nr, :nr])
                    return ins
                P.op("pe", ft, reads=[S_xrow[b], S_const], writes=[S_ps[pb]])
                src = psb[pb][:].rearrange("p (i t) -> p i t", i=4)[:, :, :nr]
                dst = xst[b][:, g * 4:(g + 1) * 4, :nr]
                if g % 2 == 0:
                    P.op("act", lambda e, src=src, dst=dst: e.copy(out=dst, in_=src), reads=[S_ps[pb]],
                         writes=[S_xst[b][g]])
                else:
                    P.op("dve", lambda e, src=src, dst=dst: e.tensor_copy(out=dst, in_=src), reads=[S_ps[pb]],
                         writes=[S_xst[b][g]])
            ld(fm(xT_a[s], r0, r0 + nr), xst[b][:, :, :nr], reads=S_xst[b], aw=[S_xTa[s]])

    P.barrier(lambda e: e.memset(eps_t[:], EPS))

    def conv_phase(cl, xsrc, S_xsrc, xdst, S_xdst, ml):
        L0 = 2 * cl
        L1 = 2 * cl + 1
        N = NB + 2 * HALO
        A = Arena(CBASE)
        xb = A.t([128, KD, N], F32)
        sq = A.t([128, KD, N], BF16)
        hT = A.t([128, KD, N], BF16)
        bufY = A.t([128, CC, N], BF16)
        cT = A.t([128, CC, NB], F32)
        wpool = [A.t([128, 8192], BF16) for _ in range(3)]
        rs = A.t([128, N], F32)
        ut = [A.t([128, N], F32) for _ in range(2)]
        sg = [A.t([128, N], F32) for _ in range(2)]
        csq = [A.t([128, NB], BF16) for _ in range(2)]
        cbf = [A.t([128, NB], BF16) for _ in range(2)]
        dg = [A.t([128, 31, 128], BF16) for _ in range(2)]
        S_dg = [Slot(), Slot()]
        idrep = A.t([128, 31, 128], BF16)
        S_idrep = Slot()

        def f_idrep(e):
            ins = None
            for t in range(31):
                ins = e.tensor_copy(out=idrep[:, t, :], in_=ident_b[:])
            return ins
        P.op("dve", f_idrep, reads=[S_const], writes=[S_idrep])
        mean = A.t([128, NB], F32)
        rstd = A.t([128, NB], F32)
        nmr = A.t([128, NB], F32)
        tn = [A.t([128, NB], F32) for _ in range(2)]
        ssl = [A.t([128, NB], F32) for _ in range(2)]
        szl = [A.t([128, NB], F32) for _ in range(2)]
        cq = A.t([128, 4, NB], F32)
        cqs = A.t([128, 4, NB], BF16)
        cqn = [A.t([128, 4, NB], BF16) for _ in range(2)]
        krt = [A.t([64, NB], F32) for _ in range(2)]
        krb = A.t([64, NB], BF16)
        cosb = A.t([64, NB], F32)
        sinb = A.t([64, NB], F32)
        yT = bufY
        yf = bufY[:, :, 0:NB]
        szb = bufY[:, 0:KD, 0:NB]

        S_xb = [Slot() for _ in range(KD)]
        S_sq, S_rs = Slot(), Slot()
        S_hT = [Slot() for _ in range(KD)]
        S_y = [Slot() for _ in range(CC)]
        S_cT = [Slot() for _ in range(CC)]
        S_w = [Slot() for _ in range(3)]
        S_ut = [Slot(), Slot()]
        S_sg = [Slot(), Slot()]
        S_csq = [Slot(), Slot()]
        S_cbf = [Slot(), Slot()]
        S_stat = Slot()
        S_tn = [Slot(), Slot()]
        S_ssl = [Slot(), Slot()]
        S_szl = [Slot(), Slot()]
        S_cq, S_cqs = Slot(), Slot()
        S_cqn = [Slot(), Slot()]
        S_krt = [Slot(), Slot()]
        S_krb, S_rope = Slot(), Slot()
        wrr = [0]
        prr = [0]

        def wnext():
            i = wrr[0] % 3
            wrr[0] += 1
            return i

        def pnext():
            i = prr[0] % 5
            prr[0] += 1
            return i

        def rms_h(l, s, src, c0, n, cols_hT):
            P.op("act", lambda e: e.activation(out=sq[:, :, 0:n], in_=src[:, :, c0:c0 + n], func=AF.Square),
                 reads=S_xb, writes=[S_sq])
            P.op("pe", lambda e: mm_group(e, psb[5][:, 0:n], [(ones[:], sq[:, k, 0:n]) for k in range(KD)]),
                 reads=[S_sq, S_const], writes=[S_ps[5]])
            P.op("act", lambda e: e.activation(out=rs[:, 0:n], in_=psb[5][:, 0:n], func=AF.Sqrt, bias=eps_t[:],
                                               scale=1.0 / D), reads=[S_ps[5], S_const], writes=[S_rs])
            P.op("dve", lambda e: e.reciprocal(out=rs[:, 0:n], in_=rs[:, 0:n]), reads=[S_rs], writes=[S_rs])
            for k in range(KD):
                b = k % 2
                P.op("dve", lambda e, k=k, b=b: e.tensor_tensor(out=ut[b][:, 0:n], in0=src[:, k, c0:c0 + n],
                                                                in1=rs[:, 0:n], op=ALU.mult),
                     reads=[S_xb[k], S_rs], writes=[S_ut[b]])
                P.op("act", lambda e, k=k, b=b: e.activation(
                    out=hT[:, k, cols_hT:cols_hT + n], in_=ut[b][:, 0:n], func=AF.Identity,
                    bias=sh[:, l, k, s:s + 1], scale=gm[:, l, k, s:s + 1]),
                    reads=[S_ut[b], S_mod], writes=[S_hT[k]])

        for s in range(2):
            for o0 in range(0, SEGT, NB):
                first = (o0 == 0)
                last = (o0 + NB == SEGT)
                g0 = s * SEGT + o0
                ld(xb[:], fm(xsrc[s], o0, o0 + N), reads=[S_xsrc[s]], writes=S_xb)
                ld(cosb[:], cos_d[:, g0:g0 + NB], writes=[S_rope])
                ld(sinb[:], sin_d[:, g0:g0 + NB], aw=[S_rope])
                rms_h(L0, s, xb, 0, N, 0)
                wcur = {}

                def issue_agl(j):
                    j2, jj = j // 2, j % 2
                    if jj == 0:
                        wi = wnext()
                        wv = wpool[wi][:].rearrange("p (h k e) -> p h k e", h=2, k=KD)
                        ld(wv[:, 0], fm(cwin_b[cl], j2 * 256, j2 * 256 + 256), reads=W_slots[("cwin", cl)],
                           writes=[S_w[wi]])
                        ld(wv[:, 1], fm(cwin_b[cl], CW + j2 * 256, CW + j2 * 256 + 256), reads=W_slots[("cwin", cl)],
                           aw=[S_w[wi]])
                        wcur["w"] = (wi, wv)
                    wi, wv = wcur["w"]
                    pa = pnext()
                    pg = pnext()
                    P.op("pe", lambda e, wv=wv, jj=jj, pa=pa: mm_group(
                        e, psb[pa][:, 0:N], [(wv[:, 0, k, jj * 128:(jj + 1) * 128], hT[:, k, :]) for k in range(KD)]),
                        reads=[S_w[wi]] + S_hT, writes=[S_ps[pa]])
                    P.op("pe", lambda e, wv=wv, jj=jj, pg=pg: mm_group(
                        e, psb[pg][:, 0:N], [(wv[:, 1, k, jj * 128:(jj + 1) * 128], hT[:, k, :]) for k in range(KD)]),
                        reads=[S_w[wi]] + S_hT, writes=[S_ps[pg]])
                    return pa, pg

                def issue_dg(j):
                    b = j % 2
                    P.op("dve", lambda e, j=j, b=b: e.tensor_tensor(
                        out=dg[b][:], in0=idrep[:],
                        in1=dw[:, cl, j, :].unsqueeze(2).to_broadcast([128, 31, 128]), op=ALU.mult),
                        reads=[S_const, S_idrep], writes=[S_dg[b]])

                issue_dg(0)
                pend = issue_agl(0)
                for j in range(CC):
                    nxt = issue_agl(j + 1) if j + 1 < CC else None
                    pa, pg = pend
                    pend = nxt
                    b = j % 2
                    P.op("act", lambda e, pg=pg, b=b: e.activation(out=sg[b][:], in_=psb[pg][:, 0:N],
                                                                   func=AF.Sigmoid),
                         reads=[S_ps[pg]], writes=[S_sg[b]])
                    P.op("dve", lambda e, pa=pa, b=b, j=j: e.tensor_tensor(out=yT[:, j, :], in0=psb[pa][:, 0:N],
                                                                           in1=sg[b][:], op=ALU.mult),
                         reads=[S_ps[pa], S_sg[b]], writes=[S_y[j]])
                    if first or last:
                        def fmask(e, j=j, first=first, last=last, s=s):
                            ins = None
                            if first:
                                ins = e.tensor_scalar(out=yT[:, j, 0:HALO], in0=yT[:, j, 0:HALO],
                                                      scalar1=hmask[:, 2 * s:2 * s + 1], scalar2=None, op0=ALU.mult)
                            if last:
                                ins = e.tensor_scalar(out=yT[:, j, N - HALO:N], in0=yT[:, j, N - HALO:N],
                                                      scalar1=hmask[:, 2 * s + 1:2 * s + 2], scalar2=None,
                                                      op0=ALU.mult)
                            return ins
                        P.op("dve", fmask, reads=[S_const], writes=[S_y[j]])
                    if j + 1 < CC:
                        issue_dg(j + 1)
                    pc = pnext()
                    P.op("pe", lambda e, j=j, b=b, pc=pc: mm_group(
                        e, psb[pc][:, 0:NB], [(dg[b][:, t, :], yT[:, j, t:t + NB]) for t in range(31)]),
                        reads=[S_dg[b], S_y[j]], writes=[S_ps[pc]])
                    P.op("act", lambda e, j=j, pc=pc: e.activation(out=cT[:, j, :], in_=psb[pc][:, 0:NB],
                                                                   func=AF.Identity, bias=dwb[:, cl, j:j + 1],
                                                                   scale=1.0),
                         reads=[S_ps[pc], S_const], writes=[S_cT[j]])
                    P.op("act", lambda e, j=j, b=b, pc=pc: e.activation(out=csq[b][:], in_=psb[pc][:, 0:NB],
                                                                        func=AF.Square, bias=dwb[:, cl, j:j + 1],
                                                                        scale=1.0),
                         reads=[S_ps[pc], S_const], writes=[S_csq[b]])
                    P.op("act", lambda e, j=j, b=b, pc=pc: e.activation(out=cbf[b][:], in_=psb[pc][:, 0:NB],
                                                                        func=AF.Identity, bias=dwb[:, cl, j:j + 1],
                                                                        scale=1.0),
                         reads=[S_ps[pc], S_const], writes=[S_cbf[b]])
                    P.op("pe", lambda e, j=j, b=b: e.matmul(psb[6][:, 0:NB], ones[:], cbf[b][:], start=(j == 0),
                                                            stop=(j == CC - 1)),
                         reads=[S_cbf[b], S_const], writes=[S_ps[6]])
                    P.op("pe", lambda e, j=j, b=b: e.matmul(psb[7][:, 0:NB], ones[:], csq[b][:], start=(j == 0),
                                                            stop=(j == CC - 1)),
                         reads=[S_csq[b], S_const], writes=[S_ps[7]])
                P.op("act", lambda e: e.activation(out=mean[:], in_=psb[6][:, 0:NB], func=AF.Copy, scale=1.0 / CW),
                     reads=[S_ps[6]], writes=[S_stat])
                P.op("dve", lambda e: e.tensor_tensor(out=nmr[:], in0=mean[:], in1=mean[:], op=ALU.mult),
                     reads=[S_stat], writes=[S_stat])
                P.op("dve", lambda e: e.scalar_tensor_tensor(out=rstd[:], in0=psb[7][:, 0:NB], scalar=1.0 / CW,
                                                             in1=nmr[:], op0=ALU.mult, op1=ALU.subtract),
                     reads=[S_stat, S_ps[7]], writes=[S_stat])
                P.op("act", lambda e: e.activation(out=rstd[:], in_=rstd[:], func=AF.Sqrt, bias=eps_t[:], scale=1.0),
                     reads=[S_stat, S_const], writes=[S_stat])
                P.op("dve", lambda e: e.reciprocal(out=rstd[:], in_=rstd[:]), reads=[S_stat], writes=[S_stat])
                P.op("dve", lambda e: e.scalar_tensor_tensor(out=nmr[:], in0=mean[:], scalar=-1.0, in1=rstd[:],
                                                             op0=ALU.mult, op1=ALU.mult),
                     reads=[S_stat], writes=[S_stat])
                for j2 in range(CC // 2):
                    wi = wnext()
                    wv = wpool[wi][:, 0:KD * 256].rearrange("p (k e) -> p k e", k=KD)
                    ld(wv, fm(cwin_b[cl], 2 * CW + j2 * 256, 2 * CW + j2 * 256 + 256), reads=W_slots[("cwin", cl)],
                       writes=[S_w[wi]])
                    for jj in range(2):
                        j = j2 * 2 + jj
                        b = j % 2
                        pz = pnext()
                        P.op("pe", lambda e, wv=wv, jj=jj, pz=pz: mm_group(
                            e, psb[pz][:, 0:NB],
                            [(wv[:, k, jj * 128:(jj + 1) * 128], hT[:, k, HALO:HALO + NB]) for k in range(KD)]),
                            reads=[S_w[wi]] + S_hT, writes=[S_ps[pz]])
                        P.op("act", lambda e, pz=pz, b=b: e.activation(out=szl[b][:], in_=psb[pz][:, 0:NB],
                                                                       func=AF.Silu),
                             reads=[S_ps[pz]], writes=[S_szl[b]])

                        def fn_(e, j=j, b=b):
                            e.tensor_tensor(out=tn[b][:], in0=cT[:, j, :], in1=rstd[:], op=ALU.mult)
                            return e.tensor_tensor(out=tn[b][:], in0=tn[b][:], in1=nmr[:], op=ALU.add)
                        P.op("dve", fn_, reads=[S_cT[j], S_stat], writes=[S_tn[b]])
                        P.op("act", lambda e, j=j, b=b: e.activation(out=ssl[b][:], in_=tn[b][:], func=AF.Silu,
                                                                     bias=lnb[:, cl, j:j + 1], scale=lng[:, cl, j:j + 1]),
                             reads=[S_tn[b], S_const], writes=[S_ssl[b]])
                        P.op("dve", lambda e, j=j, b=b: e.tensor_tensor(out=yf[:, j, :], in0=ssl[b][:], in1=szl[b][:],
                                                                        op=ALU.mult),
                             reads=[S_ssl[b], S_szl[b]], writes=[S_y[j]])
                for d2 in range(KD // 2):
                    wi = wnext()
                    wv = wpool[wi][:].rearrange("p (k e) -> p k e", k=CC)
                    ld(wv, fm(cwout_b[cl], d2 * 256, d2 * 256 + 256), reads=W_slots[("cwout", cl)], writes=[S_w[wi]])
                    for dd in range(2):
                        dc = d2 * 2 + dd
                        po = pnext()
                        P.op("pe", lambda e, wv=wv, dd=dd, po=po: mm_group(
                            e, psb[po][:, 0:NB], [(wv[:, k, dd * 128:(dd + 1) * 128], yf[:, k, :]) for k in range(CC)]),
                            reads=[S_w[wi]] + S_y, writes=[S_ps[po]])
                        P.op("dve", lambda e, dc=dc, po=po, s=s: e.scalar_tensor_tensor(
                            out=xb[:, dc, HALO:HALO + NB], in0=psb[po][:, 0:NB], scalar=gt[:, L0, dc, s:s + 1],
                            in1=xb[:, dc, HALO:HALO + NB], op0=ALU.mult, op1=ALU.add),
                            reads=[S_ps[po], S_mod], writes=[S_xb[dc]])
                ld(fm(xdst, g0, g0 + NB), xb[:, :, HALO:HALO + NB], reads=S_xb, aw=[S_xdst])
                rms_h(L1, s, xb, HALO, NB, 0)
                for part in range(2):
                    wi = wnext()
                    wv = wpool[wi][:].rearrange("p (k e) -> p k e", k=KD)
                    ld(wv, fm(mwin_b[ml], part * 512, part * 512 + 512), reads=W_slots[("mwin", ml)], writes=[S_w[wi]])
                    for rc in range(4):
                        pq = pnext()
                        P.op("pe", lambda e, wv=wv, rc=rc, pq=pq: mm_group(
                            e, psb[pq][:, 0:NB], [(wv[:, k, rc * 128:(rc + 1) * 128], hT[:, k, 0:NB]) for k in range(KD)]),
                            reads=[S_w[wi]] + S_hT, writes=[S_ps[pq]])
                        P.op("act", lambda e, rc=rc, pq=pq: e.copy(out=cq[:, rc, :], in_=psb[pq][:, 0:NB]),
                             reads=[S_ps[pq]], writes=[S_cq])
                        P.op("act", lambda e, rc=rc, pq=pq: e.activation(out=cqs[:, rc, :], in_=psb[pq][:, 0:NB],
                                                                         func=AF.Square),
                             reads=[S_ps[pq]], writes=[S_cqs])
                    P.op("pe", lambda e: mm_group(e, psb[5][:, 0:NB], [(ones[:], cqs[:, r, :]) for r in range(4)]),
                         reads=[S_cqs, S_const], writes=[S_ps[5]])
                    P.op("act", lambda e: e.activation(out=rs[:, 0:NB], in_=psb[5][:, 0:NB], func=AF.Sqrt, bias=eps_t[:],
                                                       scale=1.0 / 512), reads=[S_ps[5], S_const], writes=[S_rs])
                    P.op("dve", lambda e: e.reciprocal(out=rs[:, 0:NB], in_=rs[:, 0:NB]), reads=[S_rs], writes=[S_rs])
                    nt = qn_t if part == 0 else kvn_t
                    b = part

                    def fqn(e, nt=nt, b=b):
                        ins = None
                        for rc in range(4):
                            ins = e.scalar_tensor_tensor(out=cqn[b][:, rc, :], in0=cq[:, rc, :],
                                                         scalar=nt[:, ml, rc:rc + 1], in1=rs[:, 0:NB],
                                                         op0=ALU.mult, op1=ALU.mult)
                        return ins
                    P.op("dve", fqn, reads=[S_cq, S_rs, S_const], writes=[S_cqn[b]])
                    if part == 0:
                        ld(fm(cqT, g0, g0 + NB), cqn[0][:], reads=[S_cqn[0]], aw=[S_cqT])
                    else:
                        ld(fm(lat_in[ml][s][0:512, :], o0, o0 + NB), cqn[1][:], reads=[S_cqn[1]],
                           aw=[S_latin[ml][s]])
                wi = wnext()
                wv = wpool[wi][:, 0:KD * 128].rearrange("p (k e) -> p k e", k=KD)
                ld(wv[:, :, 0:64], fm(mwin_b[ml], 1024, 1088), reads=W_slots[("mwin", ml)], writes=[S_w[wi]])
                ld(wv[:, :, 64:96], fm(mwin_b[ml], 1056, 1088), reads=W_slots[("mwin", ml)], aw=[S_w[wi]])
                ld(wv[:, :, 96:128], fm(mwin_b[ml], 1024, 1056), reads=W_slots[("mwin", ml)], aw=[S_w[wi]])
                pa = pnext()
                pbk = pnext()
                P.op("pe", lambda e, wv=wv, pa=pa: mm_group(
                    e, psb[pa][0:64, 0:NB], [(wv[:, k, 0:64], hT[:, k, 0:NB]) for k in range(KD)]),
                    reads=[S_w[wi]] + S_hT, writes=[S_ps[pa]])
                P.op("pe", lambda e, wv=wv, pbk=pbk: mm_group(
                    e, psb[pbk][0:64, 0:NB], [(wv[:, k, 64:128], hT[:, k, 0:NB]) for k in range(KD)]),
                    reads=[S_w[wi]] + S_hT, writes=[S_ps[pbk]])
                P.op("dve", lambda e, pa=pa: e.tensor_tensor(out=krt[0][:], in0=psb[pa][0:64, 0:NB], in1=cosb[:],
                                                             op=ALU.mult), reads=[S_ps[pa], S_rope], writes=[S_krt[0]])
                P.op("dve", lambda e, pbk=pbk: e.tensor_tensor(out=krt[1][:], in0=psb[pbk][0:64, 0:NB], in1=sinb[:],
                                                               op=ALU.mult), reads=[S_ps[pbk], S_rope], writes=[S_krt[1]])
                P.op("dve", lambda e: e.tensor_tensor(out=krb[:], in0=krt[0][:], in1=krt[1][:], op=ALU.add),
                     reads=S_krt, writes=[S_krb])
                ld(lat_in[ml][s][512:576, o0:o0 + NB], krb[:], reads=[S_krb], aw=[S_latin[ml][s]])
                for z2 in range(4):
                    wi = wnext()
                    wv = wpool[wi][:].rearrange("p (k e) -> p k e", k=KD)
                    ld(wv, fm(mwin_b[ml], 1088 + z2 * 512, 1088 + z2 * 512 + 512), reads=W_slots[("mwin", ml)],
                       writes=[S_w[wi]])
                    for zz in range(4):
                        zc = z2 * 4 + zz
                        pz = pnext()
                        P.op("pe", lambda e, wv=wv, zz=zz, pz=pz: mm_group(
                            e, psb[pz][:, 0:NB], [(wv[:, k, zz * 128:(zz + 1) * 128], hT[:, k, 0:NB]) for k in range(KD)]),
                            reads=[S_w[wi]] + S_hT, writes=[S_ps[pz]])
                        P.op("act", lambda e, zc=zc, pz=pz: e.activation(out=szb[:, zc, :], in_=psb[pz][:, 0:NB],
                                                                         func=AF.Silu),
                             reads=[S_ps[pz]], writes=[S_y[zc]])
                ld(fm(szT, g0, g0 + NB), szb, reads=S_y[0:KD], aw=[S_szT])

    def attn_phase(ml):
        A = Arena(CBASE)
        Lmax = 8 * SEGT
        krT = A.t([128, Lmax], BF16)
        Kh = A.t([128, Lmax], BF16)
        Vh = A.t([128, Lmax // 128, 128], BF16)
        cosS = A.t([64, SEGT], F32)
        sinS = A.t([64, SEGT], F32)
        ckv = [A.t([128, 4, 512], BF16) for _ in range(2)]
        cqb = [A.t([128, 4, 512], BF16) for _ in range(2)]
        wkv = [A.t([128, 4, 256], BF16) for _ in range(2)]
        wuq = [A.t([128, 4, 256], BF16) for _ in range(2)]
        qnb = [A.t([128, 512], BF16) for _ in range(2)]
        qrb = [A.t([128, 512], BF16) for _ in range(2)]
        t1 = A.t([64, 512], F32)
        t2 = A.t([64, 512], F32)
        NP = 4
        pT = [A.t([128, 512], BF16) for _ in range(NP)]
        acc = [A.t([128, 512], F32) for _ in range(2)]
        rinv = A.t([128, 512], F32)
        oblk = [A.t([128, 512], BF16) for _ in range(2)]
        S_krT, S_tab = Slot(), Slot()
        S_K = [Slot() for _ in range(Lmax // 512)]
        S_V = [Slot() for _ in range(Lmax // 512)]
        S_Vones = Slot()
        S_ckv = [Slot(), Slot()]
        S_cqb = [Slot(), Slot()]
        S_wkv = [Slot(), Slot()]
        S_wuq = [Slot(), Slot()]
        S_qn = [Slot(), Slot()]
        S_qr = [Slot(), Slot()]
        S_t1, S_t2 = Slot(), Slot()
        S_pT = [Slot() for _ in range(NP)]
        S_acc = [Slot(), Slot()]
        S_rinv = Slot()
        S_oblk = [Slot(), Slot()]
        cnt = {"ckv": 0, "cq": 0, "on": 0, "ob": 0, "p": 0, "s": 0}
        hcnt = 0
        qcnt = 0

        def fzero(e):
            e.memset(krT[64:128, :], 0.0)
            e.memset(qrb[0][64:128, :], 0.0)
            return e.memset(qrb[1][64:128, :], 0.0)
        P.op("pool", fzero, writes=[S_krT, S_qr[0], S_qr[1]])
        for s in range(2):
            R = NR[s]
            L = R * SEGT
            nkb = L // 512
            nkc = L // 128
            la = lat_all[ml][s] if s == 0 else lat_sel[ml]
            for r in range(R):
                ld(krT[0:64, r * SEGT:(r + 1) * SEGT], la[r * LAT + 512:r * LAT + 576, :], reads=[S_latall[ml][s]],
                   aw=[S_krT])
            ld(cosS[:], cos_d[:, s * SEGT:(s + 1) * SEGT], writes=[S_tab])
            ld(sinS[:], sin_d[:, s * SEGT:(s + 1) * SEGT], aw=[S_tab])
            for h in range(NH):
                hb = hcnt % 2
                hcnt += 1
                ld(wkv[hb][:], fm(wukv_b[ml], h * 256, h * 256 + 256), reads=W_slots[("wukv", ml)], writes=[S_wkv[hb]])
                ld(wuq[hb][:, :, 0:192], fm(wuq_b[ml], h * 192, h * 192 + 192), reads=W_slots[("wuq", ml)],
                   writes=[S_wuq[hb]])
                ld(wuq[hb][:, :, 192:224], fm(wuq_b[ml], h * 192 + 160, h * 192 + 192), reads=W_slots[("wuq", ml)],
                   aw=[S_wuq[hb]])
                ld(wuq[hb][:, :, 224:256], fm(wuq_b[ml], h * 192 + 128, h * 192 + 160), reads=W_slots[("wuq", ml)],
                   aw=[S_wuq[hb]])
                for kb in range(nkb):
                    cb = cnt["ckv"] % 2
                    cnt["ckv"] += 1
                    r = kb // 4
                    c0 = (kb % 4) * 512
                    ld(ckv[cb][:], fm(la[r * LAT:r * LAT + 512, :], c0, c0 + 512), reads=[S_latall[ml][s]],
                       writes=[S_ckv[cb]])
                    P.op("pe", lambda e, cb=cb, hb=hb: mm_group(
                        e, psb[6][:], [(wkv[hb][:, rc, 0:128], ckv[cb][:, rc, :]) for rc in range(4)]),
                        reads=[S_wkv[hb], S_ckv[cb]], writes=[S_ps[6]])
                    P.op("act", lambda e, kb=kb: e.copy(out=Kh[:, kb * 512:(kb + 1) * 512], in_=psb[6][:]),
                         reads=[S_ps[6]], writes=[S_K[kb]])

                    def fv(e, cb=cb, hb=hb):
                        ins = None
                        for sub in range(4):
                            ins = mm_group(e, psb[7][:, sub * 128:(sub + 1) * 128],
                                           [(ckv[cb][:, rc, sub * 128:(sub + 1) * 128], wkv[hb][:, rc, 128:256])
                                            for rc in range(4)])
                        return ins
                    P.op("pe", fv, reads=[S_wkv[hb], S_ckv[cb]], writes=[S_ps[7]])
                    P.op("dve", lambda e, kb=kb: e.tensor_copy(
                        out=Vh[:, kb * 4:(kb + 1) * 4, 0:128], in_=psb[7][:].rearrange("p (a d) -> p a d", a=4)),
                        reads=[S_ps[7]], writes=[S_V[kb]])
                for qb in range(SEGT // QB):
                    g0 = s * SEGT + qb * QB
                    qi = cnt["cq"] % 2
                    cnt["cq"] += 1
                    ld(cqb[qi][:], fm(cqT, g0, g0 + QB), reads=[S_cqT], writes=[S_cqb[qi]])
                    P.op("pe", lambda e, qi=qi, hb=hb: mm_group(
                        e, psb[6][:], [(wuq[hb][:, rc, 0:128], cqb[qi][:, rc, :]) for rc in range(4)]),
                        reads=[S_wuq[hb], S_cqb[qi]], writes=[S_ps[6]])
                    P.op("act", lambda e, qi=qi: e.copy(out=qnb[qi][:], in_=psb[6][:]), reads=[S_ps[6]],
                         writes=[S_qn[qi]])
                    P.op("pe", lambda e, qi=qi, hb=hb: mm_group(
                        e, psb[7][0:64, :], [(wuq[hb][:, rc, 128:192], cqb[qi][:, rc, :]) for rc in range(4)]),
                        reads=[S_wuq[hb], S_cqb[qi]], writes=[S_ps[7]])
                    P.op("dve", lambda e, qb=qb: e.tensor_tensor(out=t1[:], in0=psb[7][0:64, :],
                                                                 in1=cosS[:, qb * QB:(qb + 1) * QB], op=ALU.mult),
                         reads=[S_ps[7], S_tab], writes=[S_t1])
                    P.op("pe", lambda e, qi=qi, hb=hb: mm_group(
                        e, psb[6][0:64, :], [(wuq[hb][:, rc, 192:256], cqb[qi][:, rc, :]) for rc in range(4)]),
                        reads=[S_wuq[hb], S_cqb[qi]], writes=[S_ps[6]])
                    P.op("dve", lambda e, qb=qb: e.tensor_tensor(out=t2[:], in0=psb[6][0:64, :],
                                                                 in1=sinS[:, qb * QB:(qb + 1) * QB], op=ALU.mult),
                         reads=[S_ps[6], S_tab], writes=[S_t2])
                    P.op("dve", lambda e, qi=qi: e.tensor_tensor(out=qrb[qi][0:64, :], in0=t1[:], in1=t2[:], op=ALU.add),
                         reads=[S_t1, S_t2], writes=[S_qr[qi]])

                    def f_s(e, kc, sb_, qi=qi):
                        e.matmul(psb[3 + sb_][:], Kh[:, kc * 128:(kc + 1) * 128], qnb[qi][:], start=True, stop=False)
                        return e.matmul(psb[3 + sb_][:], krT[:, kc * 128:(kc + 1) * 128], qrb[qi][:], start=False,
                                        stop=True)

                    def emit_s(kc):
                        sb_ = cnt["s"] % 3
                        cnt["s"] += 1
                        P.op("pe", lambda e, kc=kc, sb_=sb_, f=f_s: f(e, kc, sb_),
                             reads=[S_K[kc // 4], S_krT, S_qn[qi], S_qr[qi]], writes=[S_ps[3 + sb_]])
                        return sb_
                    sbs = {0: emit_s(0), 1: emit_s(1)}
                    for kc in range(nkc):
                        if kc + 2 < nkc:
                            sbs[kc + 2] = emit_s(kc + 2)
                        sb_ = sbs.pop(kc)
                        pi = cnt["p"] % NP
                        cnt["p"] += 1
                        P.op("act", lambda e, sb_=sb_, pi=pi: e.activation(out=pT[pi][:], in_=psb[3 + sb_][:],
                                                                           func=AF.Exp, scale=SCALE),
                             reads=[S_ps[3 + sb_]], writes=[S_pT[pi]])

                        po_ = qcnt % 2
                        P.op("pe", lambda e, kc=kc, pi=pi, nkc=nkc, po_=po_: e.matmul(
                            psb[po_][:], Vh[:, kc, :], pT[pi][:], start=(kc == 0), stop=(kc == nkc - 1)),
                            reads=[S_pT[pi], S_V[kc // 4]], writes=[S_ps[po_]])
                        if kc == 0:
                            P.op("dve", lambda e, pi=pi, po_=po_: e.tensor_copy(out=acc[po_][:], in_=pT[pi][:]),
                                 reads=[S_pT[pi]], writes=[S_acc[po_]])
                        else:
                            P.op("dve", lambda e, pi=pi, po_=po_: e.tensor_tensor(
                                out=acc[po_][:], in0=acc[po_][:], in1=pT[pi][:], op=ALU.add),
                                reads=[S_pT[pi]], writes=[S_acc[po_]])
                    ob = cnt["ob"] % 2
                    cnt["ob"] += 1
                    po_ = qcnt % 2
                    qcnt += 1
                    P.op("pe", lambda e, po_=po_: e.matmul(psb[2][:], ones_f[:], acc[po_][:], start=True, stop=True),
                         reads=[S_acc[po_], S_const], writes=[S_ps[2]])
                    P.op("dve", lambda e: e.reciprocal(out=rinv[:], in_=psb[2][:]), reads=[S_ps[2]], writes=[S_rinv])
                    P.op("dve", lambda e, ob=ob, po_=po_: e.tensor_tensor(out=oblk[ob][:], in0=psb[po_][:], in1=rinv[:],
                                                                         op=ALU.mult),
                         reads=[S_ps[po_], S_rinv], writes=[S_oblk[ob]])
                    ld(oT[h * 128:(h + 1) * 128, g0:g0 + QB], oblk[ob][:], reads=[S_oblk[ob]], aw=[S_oT])

    def oproj_phase(ml, xsrc, S_xsrc, final):
        L1 = 2 * ml + 1
        A = Arena(CBASE)
        ob = A.t([128, KD, QB], BF16)
        zb = A.t([128, KD, QB], BF16)
        xb = A.t([128, KD, QB], F32)
        wt = [A.t([128, KD, 256], BF16) for _ in range(3)]
        sq = A.t([128, KD, QB], BF16)
        rs = A.t([128, QB], F32)
        orow = [A.t([128, D], F32) for _ in range(2)]
        S_ob, S_zb, S_sq, S_rs = Slot(), Slot(), Slot(), Slot()
        S_xb = [Slot() for _ in range(KD)]
        S_wt = [Slot() for _ in range(3)]
        S_orow = [Slot(), Slot()]
        wr = 0
        pr = 0
        orr = 0
        for s in range(2):
            for tb in range(SEGT // QB):
                g0 = s * SEGT + tb * QB
                ld(ob[:], fm(oT, g0, g0 + QB), reads=[S_oT], writes=[S_ob])
                ld(zb[:], fm(szT, g0, g0 + QB), reads=[S_szT], writes=[S_zb])
                ld(xb[:], fm(xsrc, g0, g0 + QB), reads=[S_xsrc], writes=S_xb)
                P.op("dve", lambda e: e.tensor_tensor(out=ob[:], in0=ob[:], in1=zb[:], op=ALU.mult),
                     reads=[S_ob, S_zb], writes=[S_ob])
                for d2 in range(KD // 2):
                    wi = wr % 3
                    wr += 1
                    ld(wt[wi][:], fm(mwout_b[ml], d2 * 256, d2 * 256 + 256), reads=W_slots[("mwout", ml)],
                       writes=[S_wt[wi]])
                    for dd in range(2):
                        dc = d2 * 2 + dd
                        po = pr % 6
                        pr += 1
                        P.op("pe", lambda e, wi=wi, dd=dd, po=po: mm_group(
                            e, psb[po][:], [(wt[wi][:, k, dd * 128:(dd + 1) * 128], ob[:, k, :]) for k in range(KD)]),
                            reads=[S_wt[wi], S_ob], writes=[S_ps[po]])
                        P.op("dve", lambda e, dc=dc, po=po, s=s: e.scalar_tensor_tensor(
                            out=xb[:, dc, :], in0=psb[po][:], scalar=gt[:, L1, dc, s:s + 1], in1=xb[:, dc, :],
                            op0=ALU.mult, op1=ALU.add), reads=[S_ps[po], S_mod], writes=[S_xb[dc]])
                if not final:
                    ld(fm(xT_b[s], HALO + tb * QB, HALO + tb * QB + QB), xb[:], reads=S_xb, aw=[S_xTb[s]])
                    if tb == 0:
                        ld(fm(edge_in[s], 0, HALO), xb[:, :, 0:HALO], reads=S_xb, aw=[S_edgein[s]])
                    if tb == SEGT // QB - 1:
                        ld(fm(edge_in[s], HALO, 2 * HALO), xb[:, :, QB - HALO:QB], reads=S_xb, aw=[S_edgein[s]])
                else:
                    P.op("act", lambda e: e.activation(out=sq[:], in_=xb[:], func=AF.Square), reads=S_xb, writes=[S_sq])
                    P.op("pe", lambda e: mm_group(e, psb[6][:], [(ones[:], sq[:, k, :]) for k in range(KD)]),
                         reads=[S_sq, S_const], writes=[S_ps[6]])
                    P.op("act", lambda e: e.activation(out=rs[:], in_=psb[6][:], func=AF.Sqrt, bias=eps_t[:],
                                                       scale=1.0 / D), reads=[S_ps[6], S_const], writes=[S_rs])
                    P.op("dve", lambda e: e.reciprocal(out=rs[:], in_=rs[:]), reads=[S_rs], writes=[S_rs])
                    for k in range(KD):
                        P.op("dve", lambda e, k=k: e.scalar_tensor_tensor(
                            out=xb[:, k, :], in0=xb[:, k, :], scalar=fg[:, k:k + 1], in1=rs[:], op0=ALU.mult,
                            op1=ALU.mult), reads=[S_xb[k], S_rs, S_const], writes=[S_xb[k]])
                    for tt in range(QB // 128):
                        oi = orr % 2
                        orr += 1
                        for g in range(4):
                            po = pr % 6
                            pr += 1

                            def ftr(e, g=g, tt=tt, po=po):
                                ins = None
                                for i in range(4):
                                    k = g * 4 + i
                                    ins = e.transpose(out=psb[po][:, i * 128:(i + 1) * 128],
                                                      in_=xb[:, k, tt * 128:(tt + 1) * 128], identity=ident[:])
                                return ins
                            P.op("pe", ftr, reads=S_xb[g * 4:g * 4 + 4] + [S_const], writes=[S_ps[po]])
                            if g % 2 == 0:
                                P.op("act", lambda e, g=g, oi=oi, po=po: e.copy(out=orow[oi][:, g * 512:(g + 1) * 512],
                                                                                in_=psb[po][:]),
                                     reads=[S_ps[po]], writes=[S_orow[oi]])
                            else:
                                P.op("dve", lambda e, g=g, oi=oi, po=po: e.tensor_copy(
                                    out=orow[oi][:, g * 512:(g + 1) * 512], in_=psb[po][:]),
                                    reads=[S_ps[po]], writes=[S_orow[oi]])
                        r0 = tb * QB + tt * 128
                        ld(yout[s, r0:r0 + 128, :], orow[oi][:], reads=[S_orow[oi]])

    S_x1T, S_x3T, S_cqT, S_szT, S_oT = Slot(), Slot(), Slot(), Slot(), Slot()
    S_latin = [[Slot(), Slot()], [Slot(), Slot()]]
    S_latall = [[Slot(), Slot()], [Slot(), Slot()]]
    S_xTb = [Slot(), Slot()]
    S_edgein = [Slot(), Slot()]
    S_edgeall = [Slot(), Slot()]

    def gather_lat(ml):
        for s in range(2):
            P.op("pool", lambda e, s=s: e.collective_compute(
                "AllGather", ALU.bypass, replica_groups=GROUPS[s], ins=[lat_in[ml][s].opt()],
                outs=[lat_all[ml][s].opt()]), reads=[S_latin[ml][s]], writes=[S_latall[ml][s]])
        P.barrier(nopb)
        AS = Arena(CBASE)
        FL = LAT * SEGT // 128
        ta = [AS.t([128, FL], BF16) for _ in range(2)]
        tb_ = [AS.t([128, FL], BF16) for _ in range(2)]
        S_ta = [Slot(), Slot()]
        S_tb = [Slot(), Slot()]
        S_sel = Slot()

        def flat(ap):
            return ap.rearrange("r (q t) -> (r q) t", q=2).rearrange("(p a) t -> p (a t)", p=128)
        for g in range(4):
            b = g % 2
            va = flat(lat_all[ml][1][g * LAT:(g + 1) * LAT, :])
            vb = flat(lat_all[ml][1][(4 + g) * LAT:(5 + g) * LAT, :])
            vo = flat(lat_sel[ml][g * LAT:(g + 1) * LAT, :])
            ld(ta[b][:], va, reads=[S_latall[ml][1]], writes=[S_ta[b]])
            ld(tb_[b][:], vb, reads=[S_latall[ml][1]], writes=[S_tb[b]])

            def fsel(e, b=b):
                e.tensor_scalar(out=ta[b][:], in0=ta[b][:], scalar1=selp[:, 0:1], scalar2=None, op0=ALU.mult)
                return e.scalar_tensor_tensor(out=ta[b][:], in0=tb_[b][:], scalar=selp[:, 1:2], in1=ta[b][:],
                                              op0=ALU.mult, op1=ALU.add)
            P.op("dve", fsel, reads=[S_tb[b], S_const], writes=[S_ta[b]])
            ld(vo, ta[b][:], reads=[S_ta[b]], aw=[S_sel])
        S_latall[ml][1] = S_sel

    def nopb(e):
        return e.memset(eps_t[:], EPS)

    def finish():
        import contextlib
        with contextlib.ExitStack() as es:
            sems = {e: es.enter_context(nc.semaphore("s_" + e)) for e in ENGS}
            lane_sems = {e: [es.enter_context(nc.semaphore(f"l_{e}{i}")) for i in range(n)]
                         for e, n in N_LANES.items()}
            cc_sems = [es.enter_context(nc.semaphore(f"cc{i}")) for i in range(P.n_cc)]
            with nc.Block() as block:
                P.emit(block, sems, lane_sems, cc_sems)
        return nc

    if "dbg_mod" in debug:
        dbg_mod = nc.dram_tensor("dbg_mod", [128, 3, 128], F32, kind="ExternalOutput").ap()
        ld(dbg_mod[:, 0, :], gm[:].rearrange("p l k s -> p (l k s)"), reads=[S_mod])
        ld(dbg_mod[:, 1, :], sh[:].rearrange("p l k s -> p (l k s)"), reads=[S_mod])
        ld(dbg_mod[:, 2, :], gt[:].rearrange("p l k s -> p (l k s)"), reads=[S_mod])
    if stop == "p0":
        return finish()
    conv_phase(0, xT_a, S_xTa, x1T, S_x1T, 0)
    if stop == "conv0":
        return finish()
    gather_lat(0)
    P.barrier(nopb)
    if stop == "ag0":
        return finish()
    attn_phase(0)
    P.barrier(nopb)
    if stop == "attn0":
        return finish()
    oproj_phase(0, x1T, S_x1T, final=False)
    if stop == "oproj0":
        return finish()
    for s in range(2):
        P.op("pool", lambda e, s=s: e.collective_compute(
            "AllGather", ALU.bypass, replica_groups=GROUPS[s], ins=[edge_in[s].opt()], outs=[edge_all[s].opt()]),
            reads=[S_edgein[s]], writes=[S_edgeall[s]])
    P.barrier(nopb)
    AE = Arena(CBASE)
    eg = [AE.t([128, 8, KD, 32], F32), AE.t([128, 8, KD, 32], F32)]
    hal = [[AE.t([128, KD, HALO], F32) for _ in range(2)] for _ in range(2)]
    for s in range(2):
        S_eg = Slot()
        ld(eg[s][:].rearrange("p r k e -> p (r k) e"), edge_all[s].rearrange("(a p) e -> p a e", p=128),
           reads=[S_edgeall[s]], writes=[S_eg])
        for side in range(2):
            c0 = HALO if side == 0 else 0
            S_h = Slot()

            def fh(e, s=s, side=side, c0=c0):
                ins = e.tensor_scalar(out=hal[s][side][:], in0=eg[s][:, 0, :, c0:c0 + HALO],
                                      scalar1=selb[:, s, side, 0:1], scalar2=None, op0=ALU.mult)
                for r in range(1, 8):
                    ins = e.scalar_tensor_tensor(out=hal[s][side][:], in0=eg[s][:, r, :, c0:c0 + HALO],
                                                 scalar=selb[:, s, side, r:r + 1], in1=hal[s][side][:],
                                                 op0=ALU.mult, op1=ALU.add)
                return ins
            P.op("dve", fh, reads=[S_eg, S_const], writes=[S_h])
            d0 = 0 if side == 0 else HALO + SEGT
            ld(fm(xT_b[s], d0, d0 + HALO), hal[s][side][:], reads=[S_h], aw=[S_xTb[s]])
    P.barrier(nopb)
    if stop == "halo":
        return finish()
    conv_phase(1, xT_b, S_xTb, x3T, S_x3T, 1)
    if stop == "conv1":
        return finish()
    gather_lat(1)
    P.barrier(nopb)
    attn_phase(1)
    P.barrier(nopb)
    if stop == "attn1":
        return finish()
    oproj_phase(1, x3T, S_x3T, final=True)
    return finish()


def _fmv(v):
    v = np.asarray(v, np.float32)
    lead = v.shape[:-1]
    k = v.shape[-1] // 128
    return np.ascontiguousarray(np.moveaxis(v.reshape(*lead, k, 128), -1, 0))


_NC_CACHE = {}


def kernel(x_prompt, x_sample, c_prompt, c_sample, norm_g, ada_w, ada_b,
           conv_w_in, conv_dw_w, conv_dw_b, conv_ln_g, conv_ln_b, conv_w_out,
           mla_w_in, mla_q_norm, mla_kv_norm, mla_w_uq, mla_w_ukv, mla_w_out, final_g):
    f32 = np.float32
    x_prompt = np.asarray(x_prompt, f32)
    x_sample = np.asarray(x_sample, f32)
    if "nc" not in _NC_CACHE and not _NC_CACHE.get("prepare_only"):
        _NC_CACHE["nc"] = build()
    nc = _NC_CACHE.get("nc")

    def padded(seq, start):
        out = np.zeros((PADT, D), f32)
        lo = start - HALO
        hi = start + SEGT + HALO
        a = max(lo, 0)
        b = min(hi, seq.shape[0])
        out[a - lo:b - lo] = seq[a:b]
        return out

    inv = (1.0 / (np.float32(10000.0) ** (np.arange(0, 64, 2, dtype=f32) / np.float32(64)))).astype(f32)
    cs3 = np.stack([np.asarray(c_sample, f32)[0], np.asarray(c_prompt, f32)[0], np.asarray(c_prompt, f32)[1]], 0)
    cT3 = np.ascontiguousarray(_fmv(cs3).transpose(0, 2, 1))
    ng = _fmv(norm_g)
    fg = _fmv(final_g)
    dw = np.ascontiguousarray(_fmv(conv_dw_w).transpose(0, 1, 3, 2))
    dwb = _fmv(conv_dw_b)
    lng = _fmv(conv_ln_g)
    lnb = _fmv(conv_ln_b)
    qn = _fmv(mla_q_norm)
    kvn = _fmv(mla_kv_norm)
    adab_full = _fmv(ada_b)
    ada_w = np.asarray(ada_w, f32)
    shared = {
        "ng": ng, "fg": fg, "dw": dw, "dwb": dwb, "lng": lng, "lnb": lnb, "qn": qn, "kvn": kvn, "cT3": cT3,
    }
    wfull = {"conv_w_in": np.asarray(conv_w_in, f32), "conv_w_out": np.asarray(conv_w_out, f32),
             "mla_w_in": np.asarray(mla_w_in, f32), "mla_w_uq": np.asarray(mla_w_uq, f32),
             "mla_w_ukv": np.asarray(mla_w_ukv, f32), "mla_w_out": np.asarray(mla_w_out, f32)}
    in_maps = []
    for c in range(NCORES):
        b, g = c // 4, c % 4
        xin = np.stack([padded(x_sample[0], SEGT * c), padded(x_prompt[b], SEGT * g)], 0)
        pos = np.concatenate([SEGT * c + np.arange(SEGT), SEGT * g + np.arange(SEGT)]).astype(f32)
        ang = (pos[:, None] * inv[None, :]).astype(f32)
        cs = np.cos(ang).astype(f32).T
        sn = np.sin(ang).astype(f32).T
        hm = np.array([c > 0, c < 7, g > 0, g < 3], f32)
        sel = np.zeros((2, 2, 8), f32)
        if c > 0:
            sel[0, 0, c - 1] = 1
        if c < 7:
            sel[0, 1, c + 1] = 1
        if g > 0:
            sel[1, 0, c - 1] = 1
        if g < 3:
            sel[1, 1, c + 1] = 1
        sp = np.zeros(2, f32)
        sp[b] = 1
        m = dict(shared)
        for wk, wv_ in wfull.items():
            rs_ = wv_.shape[1] // 8
            m[wk] = np.ascontiguousarray(wv_[:, rs_ * c:rs_ * (c + 1), :])
        m.update({
            "xin": xin,
            "adaw": np.ascontiguousarray(ada_w[:, :, 768 * c:768 * (c + 1)]),
            "adab": np.ascontiguousarray(adab_full[:, :, 6 * c:6 * c + 6]),
            "hmask": np.ascontiguousarray(np.broadcast_to(hm, (128, 4))),
            "selb": np.ascontiguousarray(np.broadcast_to(sel, (128, 2, 2, 8))),
            "selp": np.ascontiguousarray(np.broadcast_to(sp, (128, 2))),
            "cos2": np.ascontiguousarray(np.concatenate([cs, cs], 0)),
            "sin2s": np.ascontiguousarray(np.concatenate([-sn, sn], 0)),
        })
        in_maps.append(m)
    if _NC_CACHE.get("prepare_only"):
        return in_maps
    res = run_bass_kernel_spmd(nc, in_maps, core_ids=list(range(NCORES)))
    y_sample = np.zeros((1, 8 * SEGT, D), f32)
    y_prompt = np.zeros((2, 4 * SEGT, D), f32)
    for c in range(NCORES):
        yo = np.asarray(res.results[c]["yout"], f32)
        y_sample[0, SEGT * c:SEGT * (c + 1)] = yo[0]
        y_prompt[c // 4, SEGT * (c % 4):SEGT * (c % 4 + 1)] = yo[1]
    return (y_prompt, y_sample)
```
